# Optimizing a Trainium2 kernel written in Bass

```python
import math
import jax, jax.numpy as jnp
from jax import lax
import numpy as np

D_MODEL = 1024
BATCH = 16
SEQ = 4096
DEPTH = 2
DEC_BATCH = 1
DEC_SEQ = 16384
PAST_LEN = 128

HEAD_DIM = 64
H_NA = 4
H_DIL = 6
H_GDN = 6
W_NA = H_NA * HEAD_DIM
W_DIL = H_DIL * HEAD_DIM
W_GDN = H_GDN * HEAD_DIM
GRID_W = 64
NA_KH = 8
NA_KW = 16
NA_QB = 16
NA_BAND = 32
DIL_PATTERN = ((128, 1), (512, 4), (2048, 16))
DIL_HPG = H_DIL // len(DIL_PATTERN)
BAND_Q = 64
ROPE_THETA = 10000.0
GDN_CHUNK = 64
CONV_K = 5
N_EXPERTS = 16
D_EXPERT = 2048
EC_FACTOR = 2
N_BRANCH = 3
D_IN = 3 * W_NA + 3 * W_DIL + 4 * W_GDN + 4 * H_GDN + N_BRANCH * D_MODEL
DN_ALPHA = (2 * DEPTH) ** 0.25
DN_BETA = (8 * DEPTH) ** -0.25
LN_EPS = 1e-5
NORM_EPS = 1e-6

kernel_name = 'hybrid_natten_longnet_gdn_ec_encoder'


def layer_norm(x, g, b):
    xf = x.astype(jnp.float32)
    mu = jnp.mean(xf, axis=-1, keepdims=True)
    var = jnp.mean(jnp.square(xf - mu), axis=-1, keepdims=True)
    return ((xf - mu) * lax.rsqrt(var + LN_EPS) * g + b).astype(x.dtype)


def rope(x, pos):
    half = x.shape[-1] // 2
    inv = ROPE_THETA ** (-jnp.arange(half, dtype=jnp.float32) / half)
    ang = pos.astype(jnp.float32)[:, None] * inv[None, :]
    cos, sin = jnp.cos(ang)[:, None, :], jnp.sin(ang)[:, None, :]
    x1, x2 = x[..., :half], x[..., half:]
    return jnp.concatenate([x1 * cos - x2 * sin, x1 * sin + x2 * cos], axis=-1)


def neighbourhood_attention(q, k, v, rpb):
    b, s, h, dh = q.shape
    rows = s // GRID_W
    kh = min(NA_KH, rows)
    n_cb = GRID_W // NA_QB
    r = jnp.arange(rows)
    row_idx = jnp.clip(r - kh // 2, 0, rows - kh)[:, None] + jnp.arange(kh)[None, :]
    band_start = jnp.clip(jnp.arange(n_cb) * NA_QB - (NA_BAND - NA_QB) // 2, 0, GRID_W - NA_BAND)
    col_idx = band_start[:, None] + jnp.arange(NA_BAND)[None, :]
    qc = jnp.arange(n_cb)[:, None] * NA_QB + jnp.arange(NA_QB)[None, :]
    win_start = jnp.clip(qc - NA_KW // 2, 0, GRID_W - NA_KW)
    kcol = col_idx[:, None, :]
    col_ok = (kcol >= win_start[:, :, None]) & (kcol < win_start[:, :, None] + NA_KW)
    dr = row_idx - r[:, None] + (NA_KH - 1)
    dc = jnp.clip(kcol - qc[:, :, None], -(NA_KW - 1), NA_KW - 1) + (NA_KW - 1)
    bias = rpb[:, dr[:, None, None, :, None], dc[None, :, :, None, :]]
    qg = q.astype(jnp.float32).reshape(b, rows, n_cb, NA_QB, h, dh) * dh ** -0.5
    kg = k.astype(jnp.float32).reshape(b, rows, GRID_W, h, dh)
    vg = v.astype(jnp.float32).reshape(b, rows, GRID_W, h, dh)
    gi_r, gi_c = row_idx[:, None, :, None], col_idx[None, :, None, :]
    kn = kg[:, gi_r, gi_c]
    vn = vg[:, gi_r, gi_c]
    sc = jnp.einsum('brjqhd,brjiwhd->bhrjqiw', qg, kn) + bias
    sc = jnp.where(col_ok[:, :, None, :], sc, -jnp.inf)
    p = jax.nn.softmax(sc, axis=(-2, -1))
    o = jnp.einsum('bhrjqiw,brjiwhd->brjqhd', p, vn)
    return o.reshape(b, s, h * dh)


def banded_attention(q, k, v, half):
    n, L, h, dh = q.shape
    qb = min(BAND_Q, L)
    nb = -(-L // qb)
    Lp = nb * qb
    width = qb + 2 * half
    qp = jnp.pad(q.astype(jnp.float32), ((0, 0), (0, Lp - L), (0, 0), (0, 0))).reshape(n, nb, qb, h, dh)
    pad_kv = ((0, 0), (half, Lp - L + half), (0, 0), (0, 0))
    kp = jnp.pad(k.astype(jnp.float32), pad_kv)
    vp = jnp.pad(v.astype(jnp.float32), pad_kv)
    starts = jnp.arange(nb) * qb
    kidx = starts[:, None] + jnp.arange(width)[None, :]
    kb = kp[:, kidx]
    vb = vp[:, kidx]
    qpos = starts[:, None] + jnp.arange(qb)[None, :]
    kpos = (kidx - half)[:, None, :]
    valid = (jnp.abs(qpos[:, :, None] - kpos) <= half) & (kpos >= 0) & (kpos < L)
    sc = jnp.einsum('nbqhd,nbkhd->nhbqk', qp, kb) * dh ** -0.5
    sc = jnp.where(valid, sc, -jnp.inf)
    m = jnp.max(sc, axis=-1, keepdims=True)
    p = jnp.exp(sc - m)
    den = jnp.sum(p, axis=-1)
    den_t = jnp.moveaxis(den, 1, -1)
    o = jnp.einsum('nhbqk,nbkhd->nbqhd', p, vb) / den_t[..., None]
    o = o.reshape(n, Lp, h, dh)[:, :L]
    m_t = jnp.moveaxis(m[..., 0], 1, -1).reshape(n, Lp, h)[:, :L]
    return o, m_t, den_t.reshape(n, Lp, h)[:, :L]


def dilated_attention(q, k, v):
    b, s, h, dh = q.shape
    outs, maxes, dens = [], [], []
    for g, (win, dil) in enumerate(DIL_PATTERN):
        lo, hi = g * DIL_HPG, (g + 1) * DIL_HPG
        half = win // (2 * dil)
        L = s // dil

        def to_classes(t):
            t = t[:, :, lo:hi].reshape(b, L, dil, DIL_HPG, dh)
            return jnp.swapaxes(t, 1, 2).reshape(b * dil, L, DIL_HPG, dh)

        def from_classes(t):
            rest = t.shape[2:]
            return jnp.swapaxes(t.reshape(b, dil, L, *rest), 1, 2).reshape(b, s, *rest)

        o, m, den = banded_attention(to_classes(q), to_classes(k), to_classes(v), half)
        outs.append(from_classes(o))
        maxes.append(from_classes(m))
        dens.append(from_classes(den))
    o = jnp.stack(outs)
    m = jnp.stack(maxes)
    den = jnp.stack(dens)
    wgt = den * jnp.exp(m - jnp.max(m, axis=0))
    o = jnp.sum(wgt[..., None] * o, axis=0) / jnp.sum(wgt, axis=0)[..., None]
    return o.reshape(b, s, DIL_HPG * dh)


def centred_depthwise_conv(x, w):
    c = x.shape[-1]
    return lax.conv_general_dilated(x, w[:, None, :].astype(x.dtype), window_strides=(1,),
                                    padding=[(CONV_K // 2, CONV_K // 2)],
                                    dimension_numbers=('NWC', 'WIO', 'NWC'), feature_group_count=c)


def l2_normalize(x):
    return x * lax.rsqrt(jnp.sum(jnp.square(x), axis=-1, keepdims=True) + NORM_EPS)


def gated_delta_chunked(q, k, v, beta, g):
    b, s, h, dk = q.shape
    c = GDN_CHUNK
    n = s // c

    def chunks(t):
        return jnp.moveaxis(t.reshape(b, n, c, h, -1), 3, 1)

    q = chunks(q) * dk ** -0.5
    k = chunks(k)
    v = chunks(v)
    beta = jnp.moveaxis(beta.reshape(b, n, c, h), 3, 1)
    gc = jnp.cumsum(jnp.moveaxis(g.reshape(b, n, c, h), 3, 1), axis=-1)
    incl = jnp.tril(jnp.ones((c, c), bool))
    strict = jnp.tril(jnp.ones((c, c), bool), -1)
    diff = gc[..., :, None] - gc[..., None, :]
    decay = jnp.where(incl, jnp.exp(jnp.where(incl, diff, 0.0)), 0.0)
    kb = k * beta[..., None]
    a_mat = jnp.where(strict, jnp.einsum('bhnid,bhnjd->bhnij', kb, k) * decay, 0.0)
    eye = jnp.eye(c, dtype=a_mat.dtype)
    t_mat = lax.linalg.triangular_solve(a_mat + eye, jnp.broadcast_to(eye, a_mat.shape),
                                        left_side=True, lower=True, unit_diagonal=True)
    w = jnp.einsum('bhnij,bhnjd->bhnid', t_mat, kb * jnp.exp(gc)[..., None])
    u = jnp.einsum('bhnij,bhnjd->bhnid', t_mat, v * beta[..., None])
    intra = jnp.where(incl, jnp.einsum('bhnid,bhnjd->bhnij', q, k) * decay, 0.0)
    q_dec = q * jnp.exp(gc)[..., None]
    k_dec = k * jnp.exp(gc[..., -1:] - gc)[..., None]
    g_last = jnp.exp(gc[..., -1])

    def step(state, inp):
        q_i, k_i, u_i, w_i, intra_i, gl_i = inp
        v_new = u_i - jnp.einsum('bhcd,bhde->bhce', w_i, state)
        o_i = jnp.einsum('bhcd,bhde->bhce', q_i, state) + jnp.einsum('bhij,bhje->bhie', intra_i, v_new)
        state = state * gl_i[..., None, None] + jnp.einsum('bhcd,bhce->bhde', k_i, v_new)
        return state, o_i

    xs = tuple(jnp.moveaxis(t, 2, 0) for t in (q_dec, k_dec, u, w, intra, g_last))
    state0 = jnp.zeros((b, h, dk, v.shape[-1]), q.dtype)
    _, o = lax.scan(step, state0, xs)
    o = jnp.moveaxis(jnp.moveaxis(o, 0, 2), 1, 3)
    return o.reshape(b, s, h, -1)


def gdn_branch(q, k, v, z, b_logit, a_logit, conv_w, a_log, dt_bias, norm_w):
    b, s, _ = q.shape
    qkv = jax.nn.silu(centred_depthwise_conv(jnp.concatenate([q, k, v], axis=-1), conv_w)).astype(jnp.float32)
    q, k, v = jnp.split(qkv, 3, axis=-1)
    q = l2_normalize(q.reshape(b, s, H_GDN, HEAD_DIM))
    k = l2_normalize(k.reshape(b, s, H_GDN, HEAD_DIM))
    v = v.reshape(b, s, H_GDN, HEAD_DIM)
    beta = jax.nn.sigmoid(b_logit.astype(jnp.float32)).reshape(b, s, 2, H_GDN)
    g = -jnp.exp(a_log) * jax.nn.softplus(a_logit.astype(jnp.float32).reshape(b, s, 2, H_GDN) + dt_bias)
    o_fwd = gated_delta_chunked(q, k, v, beta[:, :, 0], g[:, :, 0])
    rev = lambda t: jnp.flip(t, axis=1)
    o_bwd = rev(gated_delta_chunked(rev(q), rev(k), rev(v), rev(beta[:, :, 1]), rev(g[:, :, 1])))
    o = o_fwd + o_bwd
    o = o * lax.rsqrt(jnp.mean(jnp.square(o), axis=-1, keepdims=True) + NORM_EPS) * norm_w
    return o.reshape(b, s, W_GDN) * jax.nn.silu(z.astype(jnp.float32))


def token_mixer(x, w_in, na_rpb, conv_w, a_log, dt_bias, gdn_norm_w, w_br_na, w_br_dil, w_br_gdn, w_out):
    b, s, _ = x.shape
    proj = jnp.einsum('bsd,de->bse', x, w_in)
    sizes = (W_NA,) * 3 + (W_DIL,) * 3 + (W_GDN,) * 4 + (2 * H_GDN, 2 * H_GDN, N_BRANCH * D_MODEL)
    offsets = [int(o) for o in np.cumsum(sizes)[:-1]]
    qa, ka, va, qd, kd, vd, qc, kc, vc, zc, bc, ac, gates = jnp.split(proj, offsets, axis=-1)
    heads = lambda t, h: t.reshape(b, s, h, HEAD_DIM)
    o_na = neighbourhood_attention(heads(qa, H_NA), heads(ka, H_NA), heads(va, H_NA), na_rpb)
    pos = jnp.arange(s)
    o_dil = dilated_attention(rope(heads(qd, H_DIL).astype(jnp.float32), pos),
                              rope(heads(kd, H_DIL).astype(jnp.float32), pos), heads(vd, H_DIL))
    o_gdn = gdn_branch(qc, kc, vc, zc, bc, ac, conv_w, a_log, dt_bias, gdn_norm_w)
    gate = jax.nn.sigmoid(gates.astype(jnp.float32)).reshape(b, s, N_BRANCH, D_MODEL)
    br_na = jnp.einsum('bsw,wd->bsd', o_na.astype(x.dtype), w_br_na)
    br_dil = jnp.einsum('bsw,wd->bsd', o_dil.astype(x.dtype), w_br_dil)
    br_gdn = jnp.einsum('bsw,wd->bsd', o_gdn.astype(x.dtype), w_br_gdn)
    merged = gate[:, :, 0] * br_na + gate[:, :, 1] * br_dil + gate[:, :, 2] * br_gdn
    return jnp.einsum('bsd,de->bse', merged.astype(x.dtype), w_out)


def expert_choice_ffn(x, w_router, w_up, w_gate, w_down):
    b, s, d = x.shape
    n = b * s
    t = x.reshape(n, d)
    aff = jax.nn.softmax(jnp.einsum('nd,de->ne', t, w_router).astype(jnp.float32), axis=-1)
    cap = EC_FACTOR * n // N_EXPERTS
    gate, idx = lax.top_k(aff.T, cap)
    xe = t[idx]
    hid = jax.nn.silu(jnp.einsum('ecd,edf->ecf', xe, w_gate)) * jnp.einsum('ecd,edf->ecf', xe, w_up)
    ye = jnp.einsum('ecf,efd->ecd', hid, w_down) * gate[..., None].astype(x.dtype)
    y = jnp.zeros_like(t).at[idx.reshape(-1)].add(ye.reshape(-1, d))
    return y.reshape(b, s, d)


def trunk(x, w_in, na_rpb, conv_w, a_log, dt_bias, gdn_norm_w, w_br_na, w_br_dil, w_br_gdn, w_out,
          ln1_g, ln1_b, w_router, w_up, w_gate, w_down, ln2_g, ln2_b):
    for l in range(DEPTH):
        mix = token_mixer(x, w_in[l], na_rpb[l], conv_w[l], a_log[l], dt_bias[l], gdn_norm_w[l],
                          w_br_na[l], w_br_dil[l], w_br_gdn[l], w_out[l])
        x = layer_norm(DN_ALPHA * x + mix, ln1_g[l], ln1_b[l])
        ffn = expert_choice_ffn(x, w_router[l], w_up[l], w_gate[l], w_down[l])
        x = layer_norm(DN_ALPHA * x + ffn, ln2_g[l], ln2_b[l])
    return x


def setup_inputs(seed: int = 0) -> dict:
    key = jax.random.key(seed)
    ks = jax.random.split(key, 20)
    f32 = jnp.float32
    nrm = lambda k, shape, scale: jax.random.normal(k, shape, f32) * scale
    w_dil_out = DIL_HPG * HEAD_DIM
    dt = jnp.exp(jax.random.uniform(ks[6], (DEPTH, 2, H_GDN), f32, math.log(1e-3), math.log(1e-1)))
    return {
        'x_prompt': nrm(ks[0], (BATCH, SEQ, D_MODEL), 1.0),
        'x_sample': nrm(ks[1], (DEC_BATCH, DEC_SEQ, D_MODEL), 1.0),
        'w_in': nrm(ks[2], (DEPTH, D_MODEL, D_IN), D_MODEL ** -0.5),
        'na_rpb': nrm(ks[3], (DEPTH, H_NA, 2 * NA_KH - 1, 2 * NA_KW - 1), 0.1),
        'conv_w': nrm(ks[4], (DEPTH, CONV_K, 3 * W_GDN), CONV_K ** -0.5),
        'a_log': jnp.log(jax.random.uniform(ks[5], (DEPTH, 2, H_GDN), f32, 1.0, 16.0)),
        'dt_bias': dt + jnp.log(-jnp.expm1(-dt)),
        'gdn_norm_w': 1.0 + nrm(ks[7], (DEPTH, HEAD_DIM), 0.01),
        'w_br_na': nrm(ks[8], (DEPTH, W_NA, D_MODEL), W_NA ** -0.5 * DN_BETA),
        'w_br_dil': nrm(ks[9], (DEPTH, w_dil_out, D_MODEL), w_dil_out ** -0.5 * DN_BETA),
        'w_br_gdn': nrm(ks[10], (DEPTH, W_GDN, D_MODEL), W_GDN ** -0.5 * DN_BETA),
        'w_out': nrm(ks[11], (DEPTH, D_MODEL, D_MODEL), D_MODEL ** -0.5 * DN_BETA),
        'ln1_g': 1.0 + nrm(ks[12], (DEPTH, D_MODEL), 0.01),
        'ln1_b': nrm(ks[13], (DEPTH, D_MODEL), 0.01),
        'w_router': nrm(ks[14], (DEPTH, D_MODEL, N_EXPERTS), D_MODEL ** -0.5),
        'w_up': nrm(ks[15], (DEPTH, N_EXPERTS, D_MODEL, D_EXPERT), D_MODEL ** -0.5),
        'w_gate': nrm(ks[16], (DEPTH, N_EXPERTS, D_MODEL, D_EXPERT), D_MODEL ** -0.5),
        'w_down': nrm(ks[17], (DEPTH, N_EXPERTS, D_EXPERT, D_MODEL), D_EXPERT ** -0.5 * DN_BETA),
        'ln2_g': 1.0 + nrm(ks[18], (DEPTH, D_MODEL), 0.01),
        'ln2_b': nrm(ks[19], (DEPTH, D_MODEL), 0.01),
    }


def reference(x_prompt, x_sample, w_in, na_rpb, conv_w, a_log, dt_bias, gdn_norm_w, w_br_na, w_br_dil,
              w_br_gdn, w_out, ln1_g, ln1_b, w_router, w_up, w_gate, w_down, ln2_g, ln2_b):
    y_prompt = trunk(x_prompt, w_in, na_rpb, conv_w, a_log, dt_bias, gdn_norm_w, w_br_na, w_br_dil, w_br_gdn,
                     w_out, ln1_g, ln1_b, w_router, w_up, w_gate, w_down, ln2_g, ln2_b)
    y_sample = trunk(x_sample, w_in, na_rpb, conv_w, a_log, dt_bias, gdn_norm_w, w_br_na, w_br_dil, w_br_gdn,
                     w_out, ln1_g, ln1_b, w_router, w_up, w_gate, w_down, ln2_g, ln2_b)
    return (y_prompt, y_sample)
```

```python
import math
import numpy as np
import ml_dtypes
import concourse.bass as bass
import concourse.mybir as mybir
from concourse.bass_utils import run_bass_kernel_spmd

F32 = mybir.dt.float32
BF16 = mybir.dt.bfloat16
I32 = mybir.dt.int32
ALU = mybir.AluOpType
AF = mybir.ActivationFunctionType

NCORES = 8
D = 1024
KC = 8
DIN = 6552
COMPUTE = ('pe', 'act', 'dve', 'pool')
NSUB = 16


class Buf:
    def __init__(self, t, name, nslots, space):
        self.t = t
        self.name = name
        self.space = space
        shape = list(t.shape)
        if space == 'dram':
            self.row = int(np.prod(shape))
        else:
            self.row = int(np.prod(shape[1:]))
        self.nslots = nslots
        self.ss = max(1, -(-self.row // nslots))
        self.lw = [None] * nslots
        self.rd = [dict() for _ in range(nslots)]

    def __getitem__(self, idx):
        return self.t[idx]

    def ap(self):
        return self.t.ap()

    def slots_of(self, ap):
        off = int(ap.offset)
        aps = list(ap.ap)
        if self.space == 'dram':
            fr = aps
            lo = off
        else:
            fr = aps[1:]
            lo = off % self.row
        neg = sum((int(c) - 1) * int(s) for s, c in fr if int(s) < 0)
        ext = sum((int(c) - 1) * abs(int(s)) for s, c in fr)
        lo = lo + neg
        hi = lo + ext + 1
        a = max(0, lo // self.ss)
        b = min(self.nslots - 1, (hi - 1) // self.ss)
        return range(a, b + 1)


class Prog:
    def __init__(self, nc, n_dma_sems=20):
        self.nc = nc
        self.engs = {'pe': nc.tensor, 'act': nc.scalar, 'dve': nc.vector, 'pool': nc.gpsimd, 'sp': nc.sync}
        self.sem = {}
        self.cnt = {}
        for e in COMPUTE:
            self.sem[e] = [nc.alloc_semaphore("prg_%s_%d" % (e, i)) for i in range(NSUB)]
            self.cnt[e] = 0
        self.dsem = []
        for i in range(n_dma_sems):
            k = 'd%d' % i
            self.sem[k] = nc.alloc_semaphore("dma_" + k)
            self.cnt[k] = 0
            self.dsem.append(k)
        self.dnext = 0
        self.seen = {e: {} for e in self.engs}
        self.bufs = {}
        self.ninstr = 0
        self.nwait = 0
        self.uid = 0
        self.stacks = []

    def reg(self, t, nslots=1, space='sbuf'):
        b = Buf(t, t.name, nslots, space)
        self.bufs[t.name] = b
        return b

    def sb(self, name, shape, dt, nslots=1):
        self.uid += 1
        cm = self.nc.sbuf_tensor("%s_%d" % (name, self.uid), list(shape), dt)
        t = self.stacks[-1].enter_context(cm)
        return self.reg(t, nslots, 'sbuf')

    def ps(self, name, shape, dt=F32, nslots=1):
        self.uid += 1
        cm = self.nc.psum_tensor("%s_%d" % (name, self.uid), list(shape), dt)
        t = self.stacks[-1].enter_context(cm)
        return self.reg(t, nslots, 'psum')

    def push(self):
        import contextlib
        st = contextlib.ExitStack()
        self.stacks.append(st)
        return st

    def pop(self):
        self.barrier()
        st = self.stacks.pop()
        st.close()

    def barrier(self):
        for e in self.engs:
            self.wait_all(e)

    def dram(self, name, shape, dt, nslots=1, kind=None):
        if kind is None:
            t = self.nc.dram_tensor(name, list(shape), dt)
        else:
            t = self.nc.dram_tensor(name, list(shape), dt, kind=kind)
        return self.reg(t, nslots, 'dram')

    def _deps(self, reads, writes):
        deps = {}

        def add(k, s):
            if deps.get(k, 0) < s:
                deps[k] = s
        for ap in reads:
            b = self.bufs.get(ap.tensor.name)
            if b is None:
                continue
            for s in b.slots_of(ap):
                if b.lw[s] is not None:
                    add(*b.lw[s])
        for ap in writes:
            b = self.bufs.get(ap.tensor.name)
            if b is None:
                continue
            for s in b.slots_of(ap):
                if b.lw[s] is not None:
                    add(*b.lw[s])
                for k, q in b.rd[s].items():
                    add(k, q)
        return deps

    def _mark(self, reads, writes, key, seq):
        for ap in reads:
            b = self.bufs.get(ap.tensor.name)
            if b is None:
                continue
            for s in b.slots_of(ap):
                if b.rd[s].get(key, 0) < seq:
                    b.rd[s][key] = seq
        for ap in writes:
            b = self.bufs.get(ap.tensor.name)
            if b is None:
                continue
            for s in b.slots_of(ap):
                b.lw[s] = (key, seq)
                b.rd[s] = {}

    def _emit_waits(self, e, deps):
        eng = self.engs[e]
        for k, s in deps.items():
            if k == e and e == 'pe':
                continue
            if self.seen[e].get(k, 0) >= s:
                continue
            if k in COMPUTE:
                i = s - 1
                eng.wait_ge(self.sem[k][i % NSUB], i // NSUB + 1)
            else:
                eng.wait_ge(self.sem[k], s)
            self.seen[e][k] = s
            self.nwait += 1

    def op(self, e, fn, reads, writes):
        deps = self._deps(reads, writes)
        self._emit_waits(e, deps)
        ins = fn(self.engs[e])
        i = self.cnt[e]
        self.cnt[e] += 1
        ins.then_inc(self.sem[e][i % NSUB], 1)
        self._mark(reads, writes, e, self.cnt[e])
        self.ninstr += 1
        return ins

    def dma(self, q, out, in_, fn=None, extra_reads=(), inc=16, **kw):
        reads = [in_] + list(extra_reads)
        writes = [out]
        deps = self._deps(reads, writes)
        k = self.dsem[self.dnext]
        self.dnext = (self.dnext + 1) % len(self.dsem)
        if self.cnt[k] > 0:
            deps[k] = max(deps.get(k, 0), self.cnt[k])
        self._emit_waits(q, deps)
        eng = self.engs[q]
        if fn is None:
            ins = eng.dma_start(out=out, in_=in_, **kw)
        else:
            ins = fn(eng)
        self.cnt[k] += inc
        ins.then_inc(self.sem[k], inc)
        self._mark(reads, writes, k, self.cnt[k])
        self.ninstr += 1
        return ins

    def wait_all(self, e='sp'):
        eng = self.engs[e]
        for k, v in self.cnt.items():
            if v <= 0 or k == e:
                continue
            if k in COMPUTE:
                i = v - 1
                eng.wait_ge(self.sem[k][i % NSUB], i // NSUB + 1)
            else:
                eng.wait_ge(self.sem[k], v)
            self.seen[e][k] = v

    def mm(self, out, lhsT, rhs, start=True, stop=True):
        return self.op('pe', lambda g: g.matmul(out, lhsT, rhs, start=start, stop=stop), [lhsT, rhs], [out])

    def tr(self, out, in_, ident):
        return self.op('pe', lambda g: g.transpose(out, in_, ident), [in_, ident], [out])

    def act(self, out, in_, func, e='act', extra=(), **kw):
        return self.op('act', lambda g: g.activation(out, in_, func, **kw), [in_] + list(extra), [out])

    def tt(self, e, out, in0, in1, op):
        return self.op(e, lambda g: g.tensor_tensor(out=out, in0=in0, in1=in1, op=op), [in0, in1], [out])

    def ts(self, e, out, in0, s1, op0, s2=None, op1=None, extra=()):
        if op1 is None:
            return self.op(e, lambda g: g.tensor_scalar(out=out, in0=in0, scalar1=s1, scalar2=None, op0=op0),
                           [in0] + list(extra), [out])
        return self.op(e, lambda g: g.tensor_scalar(out=out, in0=in0, scalar1=s1, scalar2=s2, op0=op0, op1=op1),
                       [in0] + list(extra), [out])

    def stt(self, e, out, in0, scalar, in1, op0, op1, extra=()):
        return self.op(e, lambda g: g.scalar_tensor_tensor(out=out, in0=in0, scalar=scalar, in1=in1, op0=op0, op1=op1),
                       [in0, in1] + list(extra), [out])

    def cp(self, e, out, in_):
        if e == 'act':
            return self.op('act', lambda g: g.activation(out, in_, AF.Copy), [in_], [out])
        return self.op(e, lambda g: g.tensor_copy(out=out, in_=in_), [in_], [out])

    def memset(self, e, ap, val):
        return self.op(e, lambda g: g.memset(ap, val), [], [ap])


class Cfg:
    def __init__(self, SP=4096, NP=2, SS=16384, E=16, DE=2048, L=2, stages='all', debug=False):
        self.SP, self.NP, self.SS, self.E, self.DE, self.L = SP, NP, SS, E, DE, L
        self.SO = SS // NCORES
        self.T = NP * SP + SS
        self.units = [(i * SP, SP) for i in range(NP)] + [(NP * SP, SS)]
        self.stages = stages
        self.debug = debug
        self.NOWN = NP * SP + self.SO
        self.alpha = (2 * L) ** 0.25
        self.SL = (-(-int(2 * NP * SP / E * 1.25) // 128) + 2 * SS // E // 128) * 128


FM_COLS = ([0, 128] + [256, 384] + [768, 896, 1024] + [1152, 1280, 1408] +
           [1920 + 128 * i for i in range(9)] + [3480 + 128 * i for i in range(24)])
FM_QA, FM_KA, FM_QD, FM_KD, FM_QKVC, FM_GATE = 0, 2, 4, 7, 10, 19
NFM = 43


def build(cfg):
    nc = bass.Bass("TRN2", target_bir_lowering=False)
    P = Prog(nc)
    c = cfg
    T = c.T
    def dbgk(n):
        return "ExternalOutput" if (c.debug is True or (c.debug and n in c.debug)) else None
    xp_in = P.dram("xp", [c.NP * c.SP, D], F32, kind="ExternalInput")
    xs_in = P.dram("xs", [c.SS, D], F32, kind="ExternalInput")
    win_in = P.dram("win", [c.L, D, DIN], F32, kind="ExternalInput")
    rope_in = P.dram("rope", [2, 128, c.SS], F32, kind="ExternalInput")
    rot_in = P.dram("rot", [128, 128], F32, kind="ExternalInput")
    ident_in = P.dram("ident", [128, 128], F32, kind="ExternalInput")
    keys = na_keys(c)
    nab_in = P.dram("nab", [c.L, 4, len(keys), 128, 128], F32, kind="ExternalInput")
    nam_in = P.dram("nam", [len(keys), 128, 128], F32, kind="ExternalInput")
    band_in = P.dram("band", [3, 128, 128], F32, kind="ExternalInput")
    gconst_in = P.dram("gconst", [2, 5, 128, 128], F32, kind="ExternalInput")
    convw_in = P.dram("convw", [c.L, 128, 9, 5], F32, kind="ExternalInput")
    small_in = P.dram("small", [c.L, 128, 24], F32, kind="ExternalInput")
    normw_in = P.dram("normw", [c.L, 128, 64], F32, kind="ExternalInput")
    wbr_in = P.dram("wbr", [c.L, 768, D], F32, kind="ExternalInput")
    wout_in = P.dram("wout", [c.L, D, D], F32, kind="ExternalInput")
    wr_in = P.dram("wr", [c.L, D, 16], F32, kind="ExternalInput")
    ln_in = P.dram("ln", [c.L, 4, 128, D], F32, kind="ExternalInput")
    NPt_ = c.NP * c.SP // 128
    has_moe = ('MOE' in c.stages or c.stages == 'all')
    XN = P.dram("XN", [T, D], F32, nslots=256)
    if has_moe:
        wup_in = P.dram("wup", [c.L, c.E, D, c.DE], F32, kind="ExternalInput")
        wgate_in = P.dram("wgate", [c.L, c.E, D, c.DE], F32, kind="ExternalInput")
        wdown_in = P.dram("wdown", [c.L, c.E, c.DE, D], F32, kind="ExternalInput")
        yp_out = P.dram("yp", [c.NP * c.SP, D], F32, nslots=64, kind="ExternalOutput")
        ys_out = P.dram("ys", [c.SS, D], F32, nslots=64, kind="ExternalOutput")
        XE = [P.dram("XE%d" % e, [c.SL, D], BF16, nslots=16) for e in range(c.E)]
        YE = [P.dram("YE%d" % e, [c.SL, D], F32, nslots=16) for e in range(c.E)]
        agin = P.dram("agin", [128, NPt_ * 16], F32)
        agout = P.dram("agout", [NCORES * 128, NPt_ * 16], F32)
    else:
        yp_out = ys_out = None
    FM = [P.dram("FMA", [10 * 128, T], BF16, nslots=256, kind=dbgk("FM")),
          P.dram("FMC", [9 * 128, T], BF16, nslots=256, kind=dbgk("FM")),
          P.dram("FMG", [24 * 128, T], BF16, nslots=256, kind=dbgk("FM"))]
    TMV = P.dram("TMV", [T, 1056], BF16, nslots=256, kind=dbgk("TMV"))
    BA = P.dram("BA", [T // 512, 128, 128], F32, nslots=64, kind=dbgk("BA"))
    OBRT = P.dram("OBRT", [768, T], BF16, nslots=256, kind=dbgk("OBRT"))
    OD = [P.dram("OD%d" % g, [T, 130], F32, nslots=256) for g in range(3)]
    QKT = P.dram("QKT", [768, T], BF16, nslots=256)
    KVTM = [P.dram("KVTM%d" % i, [T, 384], BF16, nslots=256) for i in range(2)]
    OFD = P.dram("OFD", [T, 384], F32, nslots=256)
    X1 = P.dram("X1", [T, D], F32, nslots=256, kind=dbgk("X1"))
    AFFD = P.dram("AFFD", [128, (T // 128) * 16], F32, kind=dbgk("AFFD"))
    out_dummy = P.dram("done", [128, 8], F32, kind="ExternalOutput")

    P.push()
    ident = P.sb("ident", [128, 128], F32)
    identb = P.sb("identb", [128, 128], BF16)
    rot = P.sb("rot", [128, 128], BF16)
    tmpc = P.sb("tmpc", [128, 128], F32)
    P.dma('sp', ident[:], ident_in[:, :])
    P.cp('dve', identb[:], ident[:])
    P.dma('sp', tmpc[:], rot_in[:, :])
    P.cp('dve', rot[:], tmpc[:])

    AFFS = P.sb("AFFS", [128, T // 128, 16], F32, nslots=T // 128)

    def xsrc0(r):
        if r < c.NP * c.SP:
            return xp_in[r:r + 128, :]
        return xs_in[r - c.NP * c.SP:r - c.NP * c.SP + 128, :]

    def xsrc1(r):
        return XN[r:r + 128, :]

    def dst_mid(r):
        return XN[r:r + 128, :]

    def dst_fin(r):
        if r < c.NP * c.SP:
            return yp_out[r:r + 128, :]
        return ys_out[r - c.NP * c.SP:r - c.NP * c.SP + 128, :]
    for l in range(c.L):
        if 'S1' in c.stages or c.stages == 'all':
            stage_s1(P, c, l, xp_in if l == 0 else None, xs_in if l == 0 else None, win_in, rope_in, FM, TMV, BA, identb, rot, XN)
        if 'NA' in c.stages or c.stages == 'all':
            stage_na(P, c, l, FM[0], TMV, OBRT, nab_in, nam_in, identb, keys)
        if 'DIL' in c.stages or c.stages == 'all':
            stage_dil(P, c, l, FM[0], TMV, OBRT, OD, band_in, identb)
        if 'GDN' in c.stages or c.stages == 'all':
            stage_gdn(P, c, l, FM[1], TMV, OBRT, QKT, KVTM, OFD, gconst_in, convw_in, small_in, normw_in, identb, ident)
        if 'MRG' in c.stages or c.stages == 'all':
            stage_merge(P, c, l, xsrc0 if l == 0 else xsrc1, FM[2], OBRT, X1, AFFS, wbr_in, wout_in, wr_in, ln_in, identb)
            if c.debug:
                P.dma('sp', AFFD[:, :], AFFS[:].rearrange("p a b -> p (a b)"))
        if has_moe:
            stage_moe(P, c, l, X1, AFFS, XE, YE, agin, agout, wup_in, wgate_in, wdown_in, ln_in, identb,
                      dst_fin if l == c.L - 1 else dst_mid)
        else:
            break

    fin = P.sb("fin", [128, 8], F32)
    P.memset('dve', fin[:], 1.0)
    P.dma('sp', out_dummy[:, :], fin[:])
    P.wait_all('sp')
    return nc, P


def stage_s1(P, c, l, xp_src, xs_src, win_in, rope_in, FM, TMV, BA, identb, rot, XN=None):
    nc = P.nc
    P.push()
    BAs = [P.sb("bas", [128, 4, 32], F32) for _ in range(2)]
    W = P.sb("W", [128, KC, DIN], BF16, nslots=KC)
    wst = [P.sb("wst", [128, DIN], F32)]
    for kc in range(KC):
        P.dma('sp', wst[0][:], win_in[l, kc * 128:(kc + 1) * 128, :])
        P.cp('dve' if kc % 2 else 'pool', W[:, kc, :], wst[0][:])
    xt = [P.sb("xt", [128, D], F32) for _ in range(4)]
    XTb = P.sb("XTb", [128, KC, 512], BF16, nslots=KC)
    ST = [P.sb("st", [128, 512], BF16) for _ in range(4)]
    TMs = [P.sb("tms", [128, 1056], BF16) for _ in range(2)]
    cs = P.sb("cos", [128, 512], F32)
    sn = P.sb("sin", [128, 512], F32)
    qb = P.sb("qb", [128, 512], BF16)
    t1 = P.sb("t1", [128, 512], F32)
    t2 = P.sb("t2", [128, 512], F32)
    PT = P.ps("PT", [128, 4, 128], BF16)
    xtb = [P.sb("xtb", [128, D], BF16) for _ in range(4)]
    PF = [P.ps("PF", [128, 512]) for _ in range(2)]
    PR = P.ps("PR", [128, 512])
    PT2 = [P.ps("PT2", [128, 512]) for _ in range(2)]
    sti = 0
    pfi = 0
    for (u0, ulen) in (c.units if 'nounits' not in c.stages else []):
        src = xs_src if u0 == c.NP * c.SP else xp_src
        srow0 = 0 if u0 == c.NP * c.SP else u0
        if xp_src is None:
            src, srow0 = XN, u0
        for b in range(ulen // 512):
            r0 = u0 + b * 512
            for t in range(4):
                P.dma('sp', xt[t][:], src[srow0 + b * 512 + t * 128: srow0 + b * 512 + (t + 1) * 128, :])
            P.dma('sp', cs[:], rope_in[0, :, b * 512:(b + 1) * 512])
            P.dma('sp', sn[:], rope_in[1, :, b * 512:(b + 1) * 512])
            for t in range(4):
                P.cp('pool' if t % 2 else 'dve', xtb[t][:], xt[t][:])
            for kc in range(KC):
                for t in range(4 if 'notr' not in c.stages else 0):
                    P.tr(PT[:, t, :], xtb[t][:, kc * 128:(kc + 1) * 128], identb[:])
                P.cp('act' if kc % 2 else 'dve', XTb[:, kc, :], PT[:].rearrange("p a b -> p (a b)"))
            for ch in range(NFM if 'nofm' not in c.stages else 0):
                col = FM_COLS[ch]
                pf = PF[pfi % 2]
                pfi += 1
                for kc in range(KC):
                    P.mm(pf[:], W[:, kc, col:col + 128], XTb[:, kc, :], start=(kc == 0), stop=(kc == KC - 1))
                st = ST[sti % 4]
                sti += 1
                if FM_QD <= ch < FM_QKVC:
                    P.cp('act', qb[:], pf[:])
                    P.mm(PR[:], rot[:], qb[:])
                    P.tt('pool', t1[:], qb[:], cs[:], ALU.mult)
                    P.tt('dve', t2[:], PR[:], sn[:], ALU.mult)
                    P.tt('pool', st[:], t1[:], t2[:], ALU.add)
                elif ch >= FM_GATE:
                    P.act(st[:], pf[:], AF.Sigmoid)
                else:
                    P.cp('dve' if ch % 2 else 'act', st[:], pf[:])
                fmt, fch = (FM[0], ch) if ch < FM_QKVC else ((FM[1], ch - FM_QKVC) if ch < FM_GATE else (FM[2], ch - FM_GATE))
                P.dma('sp', fmt[fch * 128:(fch + 1) * 128, r0:r0 + 512], st[:])
            for t in range(4 if 'notm' not in c.stages else 0):
                tm = TMs[t % 2]
                bas = BAs[(r0 // 512) % 2]
                for (c0, wd, o0) in ((512, 256, 0), (1536, 384, 256), (3072, 408, 640)):
                    p2 = PT2[pfi % 2]
                    pfi += 1
                    for kc in range(KC):
                        P.mm(p2[:, 0:wd], XTb[:, kc, t * 128:(t + 1) * 128], W[:, kc, c0:c0 + wd],
                             start=(kc == 0), stop=(kc == KC - 1))
                    if c0 == 3072:
                        P.act(tm[:, 640:1024], p2[:, 0:384], AF.Silu if 'nosilu' not in c.stages else AF.Copy)
                        P.cp("dve", tm[:, 1024:1048], p2[:, 384:408])
                    else:
                        P.cp('dve', tm[:, o0:o0 + wd], p2[:, 0:wd])
                P.dma('sp', TMV[r0 + t * 128:r0 + (t + 1) * 128, :], tm[:])

    P.pop()


def na_plan(R):
    plan = []
    for a in range(R // 2):
        st = [min(max(2 * a + qp - 4, 0), R - 8) for qp in range(2)]
        lo = min(st) // 2
        hi = (max(st) + 7) // 2
        lst = []
        for b in range(lo, hi + 1):
            valid = tuple(tuple(1 if st[qp] <= 2 * b + kp < st[qp] + 8 else 0 for qp in range(2)) for kp in range(2))
            if not any(any(v) for v in valid):
                continue
            lst.append((b, (b - a, valid)))
        plan.append(lst)
    return plan


def na_keys(cfg):
    keys = []
    for (u0, ul) in cfg.units:
        for lst in na_plan(ul // 64):
            for b, k in lst:
                if k not in keys:
                    keys.append(k)
    return keys


def na_host_tables(na_rpb, keys):
    L = na_rpb.shape[0]
    kp = np.arange(128) // 64
    kc = np.arange(128) % 64
    qp, qc = kp, kc
    ws = np.clip(qc - 8, 0, 48)
    col_ok = (kc[:, None] >= ws[None, :]) & (kc[:, None] < ws[None, :] + 16)
    dc = np.clip(kc[:, None] - qc[None, :], -15, 15) + 15
    bias = np.zeros((L, 4, len(keys), 128, 128), np.float32)
    mask = np.zeros((len(keys), 128, 128), np.float32)
    for vi, (delta, valid) in enumerate(keys):
        rd = 2 * delta + kp[:, None] - qp[None, :]
        dr = np.clip(rd + 7, 0, 14)
        v = np.array(valid)[kp[:, None], qp[None, :]].astype(bool)
        mask[vi] = (v & col_ok).astype(np.float32)
        bias[:, :, vi] = na_rpb[:, :, dr, dc]
    return bias, mask


def stage_na(P, c, l, FMA, TMV, OBRT, nab_in, nam_in, identb, keys):
    NV = len(keys)
    for hp in range(2):
        P.push()
        EB = P.sb("EB", [128, 2, NV, 128], BF16, nslots=2 * NV)
        tb = [P.sb("tb", [128, 128], F32) for _ in range(2)]
        tm_ = [P.sb("tmk", [128, 128], F32) for _ in range(2)]
        te = [P.sb("te", [128, 128], F32) for _ in range(2)]
        for j in range(2):
            for v in range(NV):
                i = (j * NV + v) % 2
                P.dma('sp', tb[i][:], nab_in[l, hp * 2 + j, v, :, :])
                P.dma('sp', tm_[i][:], nam_in[v, :, :])
                P.act(te[i][:], tb[i][:], AF.Exp)
                P.tt('dve', EB[:, j, v, :], te[i][:], tm_[i][:], ALU.mult)
        for (u0, s) in c.units:
            P.push()
            nt = s // 128
            QT = P.sb("QT", [128, s], BF16, nslots=nt // 4)
            KT = P.sb("KT", [128, s], BF16, nslots=nt // 4)
            V1 = P.sb("V1", [128, nt, 2, 65], BF16, nslots=nt // 4)
            for q in range(nt // 4):
                P.dma('sp', QT[:, q * 512:(q + 1) * 512], FMA[(FM_QA + hp) * 128:(FM_QA + hp + 1) * 128, u0 + q * 512:u0 + (q + 1) * 512])
                P.dma('sp', KT[:, q * 512:(q + 1) * 512], FMA[(FM_KA + hp) * 128:(FM_KA + hp + 1) * 128, u0 + q * 512:u0 + (q + 1) * 512])
                P.memset('pool', V1[:, q * 4:(q + 1) * 4, :, 64:65], 1.0)
                for b in range(q * 4, q * 4 + 4):
                    P.dma('sp', V1[:, b, :, 0:64],
                          TMV[u0 + b * 128:u0 + (b + 1) * 128, hp * 128:(hp + 1) * 128].rearrange("p (j d) -> p j d", j=2))
            PS = [P.ps("PS", [128, 128]) for _ in range(2)]
            PO = [P.ps("PO", [128, 2, 65]) for _ in range(2)]
            PTr = P.ps("PTr", [128, 128], BF16)
            Eb = [P.sb("Eb", [128, 128], BF16) for _ in range(3)]
            Pb = [P.sb("Pb", [128, 128], BF16) for _ in range(3)]
            rec = [P.sb("rec", [128, 2, 1], F32) for _ in range(2)]
            ob = [P.sb("ob", [128, 2, 64], BF16) for _ in range(2)]
            OST = [P.sb("OST", [128, 512], BF16) for _ in range(2)]
            plan = na_plan(s // 64)
            k = 0
            for a in range(nt):
                po = PO[a % 2]
                for j in range(2):
                    lst = plan[a]
                    for n, (b, key) in enumerate(lst):
                        v = keys.index(key)
                        ps = PS[k % 2]
                        P.mm(ps[:], KT[64 * j:64 * j + 64, b * 128:(b + 1) * 128], QT[64 * j:64 * j + 64, a * 128:(a + 1) * 128])
                        P.act(Eb[k % 3][:], ps[:], AF.Exp, scale=0.125)
                        P.tt('pool' if k % 2 else 'dve', Pb[k % 3][:], Eb[k % 3][:], EB[:, j, v, :], ALU.mult)
                        P.mm(po[:, j, :], Pb[k % 3][:], V1[:, b, j, :], start=(n == 0), stop=(n == len(lst) - 1))
                        k += 1
                r = rec[a % 2]
                P.op('dve', lambda g: g.reciprocal(out=r[:], in_=po[:, :, 64:65]), [po[:, :, 64:65]], [r[:]])
                P.tt('dve', ob[a % 2][:], po[:, :, 0:64], r[:].broadcast_to([128, 2, 64]), ALU.mult)
                P.tr(PTr[:], ob[a % 2][:].rearrange("p a b -> p (a b)"), identb[:])
                ost = OST[(a // 4) % 2]
                P.cp('act', ost[:, (a % 4) * 128:(a % 4 + 1) * 128], PTr[:])
                if a % 4 == 3:
                    P.dma('sp', OBRT[hp * 128:(hp + 1) * 128, u0 + (a - 3) * 128:u0 + (a + 1) * 128], ost[:])
            P.pop()
        P.pop()


DILS = (1, 4, 16)


def stage_dil(P, c, l, FMA, TMV, OBRT, OD, band_in, identb):
    P.push()
    BAND = P.sb("BAND", [128, 3, 128], BF16)
    tb = P.sb("tbd", [128, 3, 128], F32)
    for i in range(3):
        P.dma('sp', tb[:, i, :], band_in[i, :, :])
    P.cp('dve', BAND[:], tb[:])
    for g, d in enumerate(DILS):
        for (u0, s) in c.units:
            P.push()
            L = s // d
            nlb = L // 128
            QT = P.sb("QTd", [128, s], BF16, nslots=s // 512)
            KT = P.sb("KTd", [128, s], BF16, nslots=s // 512)
            for q in range(s // 512):
                P.dma('sp', QT[:, q * 512:(q + 1) * 512], FMA[(FM_QD + g) * 128:(FM_QD + g + 1) * 128, u0 + q * 512:u0 + (q + 1) * 512])
                P.dma('sp', KT[:, q * 512:(q + 1) * 512], FMA[(FM_KD + g) * 128:(FM_KD + g + 1) * 128, u0 + q * 512:u0 + (q + 1) * 512])
            V1 = [P.sb("V1d", [128, nlb, 2, 65], BF16, nslots=nlb) for _ in range(2)]
            for vv in V1:
                P.memset('pool', vv[:, :, :, 64:65], 1.0)
            PS = [P.ps("PSd", [128, 128]) for _ in range(2)]
            PO = [P.ps("POd", [128, 2, 65]) for _ in range(2)]
            Eb = [P.sb("Ebd", [128, 128], BF16) for _ in range(3)]
            Pb = [P.sb("Pbd", [128, 128], BF16) for _ in range(3)]
            stg = [P.sb("stgd", [128, 130], F32) for _ in range(2)]
            k = 0
            it = 0
            for cl in range(d):
                v1 = V1[cl % 2]

                def qsl(t, j, lb):
                    return t[64 * j:64 * j + 64, 128 * d * lb:128 * d * (lb + 1)].rearrange("p (i dd) -> p i dd", dd=d)[:, :, cl]
                for lb in range(nlb):
                    rows = TMV[u0 + 128 * d * lb:u0 + 128 * d * (lb + 1), 256 + g * 128:256 + (g + 1) * 128]
                    P.dma('sp', v1[:, lb, :, 0:64], rows.rearrange("(p dd) (j e) -> p dd j e", dd=d, j=2)[:, cl, :, :])
                for lb in range(nlb):
                    po = PO[it % 2]
                    tiles = [b for b in (lb - 1, lb, lb + 1) if 0 <= b < nlb]
                    for j in range(2):
                        for n, b in enumerate(tiles):
                            ps = PS[k % 2]
                            P.mm(ps[:], qsl(KT, j, b), qsl(QT, j, lb))
                            P.act(Eb[k % 3][:], ps[:], AF.Exp, scale=0.125)
                            P.tt('pool' if k % 2 else 'dve', Pb[k % 3][:], Eb[k % 3][:], BAND[:, b - lb + 1, :], ALU.mult)
                            P.mm(po[:, j, :], Pb[k % 3][:], v1[:, b, j, :], start=(n == 0), stop=(n == len(tiles) - 1))
                            k += 1
                    sg = stg[it % 2]
                    P.cp('act', sg[:], po[:].rearrange("p a b -> p (a b)"))
                    orows = OD[g][u0 + 128 * d * lb:u0 + 128 * d * (lb + 1), :].rearrange("(p dd) e -> p dd e", dd=d)[:, cl, :]
                    P.dma('sp', orows, sg[:])
                    it += 1
            P.pop()
    P.push()
    inb = [P.sb("inb", [128, 3, 130], F32) for _ in range(2)]
    acc = [P.sb("accd", [128, 130], F32) for _ in range(2)]
    rec = [P.sb("recd", [128, 2, 1], F32) for _ in range(2)]
    ob = [P.sb("obd", [128, 2, 64], BF16) for _ in range(2)]
    OST = [P.sb("OSTd", [128, 512], BF16) for _ in range(2)]
    PTr = P.ps("PTrd", [128, 128], BF16)
    for a in range(c.T // 128):
        ib = inb[a % 2]
        for g in range(3):
            P.dma('sp', ib[:, g, :], OD[g][a * 128:(a + 1) * 128, :])
        ac = acc[a % 2]
        P.tt('pool', ac[:], ib[:, 0, :], ib[:, 1, :], ALU.add)
        P.tt('pool', ac[:], ac[:], ib[:, 2, :], ALU.add)
        a3 = ac[:].rearrange("p (j e) -> p j e", j=2)
        r = rec[a % 2]
        P.op('dve', lambda gg: gg.reciprocal(out=r[:], in_=a3[:, :, 64:65]), [ac[:]], [r[:]])
        P.tt('dve', ob[a % 2][:], a3[:, :, 0:64], r[:].broadcast_to([128, 2, 64]), ALU.mult)
        P.tr(PTr[:], ob[a % 2][:].rearrange("p a b -> p (a b)"), identb[:])
        ost = OST[(a // 4) % 2]
        P.cp('act', ost[:, (a % 4) * 128:(a % 4 + 1) * 128], PTr[:])
        if a % 4 == 3:
            P.dma('sp', OBRT[256:384, (a - 3) * 128:(a + 1) * 128], ost[:])
    P.pop()
    P.pop()


def gdn_host_consts():
    t = np.arange(128)[:, None]
    i = np.arange(128)[None, :]
    out = np.zeros((2, 5, 128, 128), np.float32)
    for dr in range(2):
        inc_ji = (t <= i) if dr == 0 else (t >= i)
        st_ji = (t < i) if dr == 0 else (t > i)
        out[dr, 0] = inc_ji
        out[dr, 1] = np.where(inc_ji, 0.0, -1e4)
        out[dr, 2] = np.where(inc_ji.T, 0.0, 1e4)
        out[dr, 3] = np.where(st_ji, -1.0, 0.0)
        out[dr, 4] = np.where(st_ji.T, -1.0, 0.0)
    return out


def stage_gdn(P, c, l, FMC, TMV, OBRT, QKT, KVTM, OFD, gconst_in, convw_in, small_in, normw_in, identb, ident):
    P.push()
    nchk = c.T // 128
    GB = P.sb("GB", [128, nchk, 24], F32, nslots=nchk)
    eps6 = P.sb("eps6", [128, 1], F32)
    P.memset('pool', eps6[:], 1e-6)
    P.push()
    cw = P.sb("cw", [128, 9, 5], F32)
    P.dma('sp', cw[:], convw_in[l, :, :, :])
    DW = P.sb("DW", [128, 9, 5, 128], BF16, nslots=45)
    for ch in range(9):
        for j in range(5):
            P.ts('pool' if (ch + j) % 2 else 'dve', DW[:, ch, j, :], identb[:], cw[:, ch, j:j + 1], ALU.mult, extra=[cw[:]])
    BLK = P.sb("BLK", [128, 128], BF16)
    P.memset('pool', BLK[:], 0.0)
    P.memset('pool', BLK[0:64, 0:64], 1.0)
    P.memset('pool', BLK[64:128, 64:128], 1.0)
    sm = P.sb("sm", [128, 24], F32)
    P.dma('sp', sm[:], small_in[l, :, :])
    nega = P.sb("nega", [128, 12], F32)
    P.act(nega[:], sm[:, 0:12], AF.Exp)
    P.ts('dve', nega[:], nega[:], -1.0, ALU.mult)
    X9 = [P.sb("X9", [128, 9, 516], BF16) for _ in range(2)]
    PC = [P.ps("PC", [128, 512]) for _ in range(2)]
    PSS = P.ps("PSS", [128, 512])
    PTt = [P.ps("PTt", [128, 3, 128], BF16) for _ in range(2)]
    Y = [P.sb("Y", [128, 512], F32) for _ in range(2)]
    sq = [P.sb("sq", [128, 512], BF16) for _ in range(2)]
    rin = [P.sb("rin", [128, 512], F32) for _ in range(2)]
    YN = [P.sb("YN", [128, 9, 512], BF16, nslots=9) for _ in range(2)]
    tms = [P.sb("tmsg", [128, 384], BF16) for _ in range(4)]
    bl = [P.sb("bl", [128, 4, 24], BF16) for _ in range(2)]
    blf = [P.sb("blf", [128, 4, 24], F32) for _ in range(2)]
    ex = [P.sb("exg", [128, 4, 12], F32) for _ in range(2)]
    ex5 = [P.sb("ex5", [128, 4, 60], F32) for _ in range(2)]
    blk_i = 0
    ti = 0
    for (u0, s) in c.units:
        for b in range(s // 512):
            r0 = u0 + b * 512
            x9 = X9[blk_i % 2]
            yn = YN[blk_i % 2]
            lo = max(r0 - 2, u0)
            hi = min(r0 + 514, u0 + s)
            if lo > r0 - 2:
                P.memset('pool', x9[:, :, 0:2], 0.0)
            if hi < r0 + 514:
                P.memset('pool', x9[:, :, 514:516], 0.0)
            P.dma('sp', x9[:, :, lo - (r0 - 2):hi - (r0 - 2)], FMC[:, lo:hi].rearrange("(ch p) t -> p ch t", p=128))
            for ch in range(9):
                pc = PC[ch % 2]
                for j in range(5):
                    P.mm(pc[:], DW[:, ch, j, :], x9[:, ch, j:j + 512], start=(j == 0), stop=(j == 4))
                if ch < 6:
                    y = Y[ch % 2]
                    P.act(y[:], pc[:], AF.Silu)
                    P.tt('pool', sq[ch % 2][:], y[:], y[:], ALU.mult)
                    P.mm(PSS[:], BLK[:], sq[ch % 2][:])
                    P.act(rin[ch % 2][:], PSS[:], AF.Sqrt, bias=eps6[:], extra=[eps6[:]])
                    P.op('dve', lambda g, o=rin[ch % 2]: g.reciprocal(out=o[:], in_=o[:]), [rin[ch % 2][:]], [rin[ch % 2][:]])
                    if ch < 3:
                        P.stt('dve', yn[:, ch, :], y[:], 0.125, rin[ch % 2][:], ALU.mult, ALU.mult)
                    else:
                        P.tt('dve', yn[:, ch, :], y[:], rin[ch % 2][:], ALU.mult)
                    P.dma('sp', QKT[ch * 128:(ch + 1) * 128, r0:r0 + 512], yn[:, ch, :])
                else:
                    P.act(yn[:, ch, :], pc[:], AF.Silu)
            for t in range(4):
                for grp in range(2):
                    pt = PTt[ti % 2]
                    tm = tms[ti % 4]
                    ti += 1
                    for cc in range(3):
                        P.tr(pt[:, cc, :], yn[:, 3 + 3 * grp + cc, t * 128:(t + 1) * 128], identb[:])
                    P.cp('act' if grp else 'dve', tm[:], pt[:].rearrange("p a b -> p (a b)"))
                    P.dma('sp', KVTM[grp][r0 + t * 128:r0 + (t + 1) * 128, :], tm[:])
            bb = bl[blk_i % 2]
            bf = blf[blk_i % 2]
            e4 = ex[blk_i % 2]
            P.dma('sp', bb[:], TMV[r0:r0 + 512, 1024:1048].rearrange("(t p) c -> p t c", p=128))
            P.cp('pool', bf[:], bb[:])
            gbv = GB[:, r0 // 128:r0 // 128 + 4, :]
            P.act(gbv[:, :, 0:12], bf[:, :, 0:12], AF.Sigmoid)
            P.tt('pool', e4[:], bf[:, :, 12:24], sm[:, 12:24].unsqueeze(1).broadcast_to([128, 4, 12]), ALU.add)
            ax, yv, zv, z2, pl = [ex5[blk_i % 2][:, :, i * 12:(i + 1) * 12] for i in range(5)]
            P.ts('dve', ax, e4[:], -1.0, ALU.mult)
            P.tt('dve', ax, ax, e4[:], ALU.max)
            P.act(yv, ax, AF.Exp, scale=-1.0)
            P.ts('dve', zv, yv, 2.0, ALU.add)
            P.op('dve', lambda g, o=zv: g.reciprocal(out=o, in_=o), [zv], [zv])
            P.tt('dve', zv, zv, yv, ALU.mult)
            P.tt('dve', z2, zv, zv, ALU.mult)
            P.ts('dve', pl, z2, 1.0 / 11, ALU.mult, 1.0 / 9, ALU.add)
            for cf in (1.0 / 7, 1.0 / 5, 1.0 / 3, 1.0):
                P.tt('dve', pl, pl, z2, ALU.mult)
                P.ts('dve', pl, pl, cf, ALU.add)
            P.tt('dve', pl, pl, zv, ALU.mult)
            P.ts('dve', e4[:], e4[:], 0.0, ALU.max)
            P.stt('dve', e4[:], pl, 2.0, e4[:], ALU.mult, ALU.add)
            P.tt('pool', gbv[:, :, 12:24], e4[:], nega[:].unsqueeze(1).broadcast_to([128, 4, 12]), ALU.mult)
            blk_i += 1
    P.pop()
    P.push()
    gc_ = P.sb("gconst", [128, 2, 5, 128], F32)
    for dr in range(2):
        for i in range(5):
            P.dma('sp', gc_[:, dr, i, :], gconst_in[dr, i, :, :])
    ONES = P.sb("ONESf", [128, 128], F32)
    NONES = P.sb("NONESf", [128, 128], F32)
    ONESb = P.sb("ONESb", [128, 128], BF16)
    P.memset('pool', ONES[:], 1.0)
    P.memset('pool', NONES[:], -1.0)
    P.memset('pool', ONESb[:], 1.0)
    nw = P.sb("nw", [128, 64], F32)
    P.dma('sp', nw[:], normw_in[l, :, :])
    S = P.sb("S", [64, 6, 64], F32)
    Sb = P.sb("Sb", [64, 6, 64], BF16)
    PA = [P.ps("PA", [128, 4, 128]) for _ in range(5)]
    PB = [P.ps("PB", [128, 6, 64]) for _ in range(2)]

    def B3(ap2, n=6):
        return ap2.unsqueeze(1).broadcast_to([128, n, 128])

    qT = [P.sb("qTg", [64, 6, 128], BF16) for _ in range(2)]
    kT = [P.sb("kTg", [64, 6, 128], BF16) for _ in range(2)]
    ktm = [P.sb("ktm", [128, 6, 64], BF16) for _ in range(2)]
    vtm = [P.sb("vtm", [128, 6, 64], BF16) for _ in range(2)]
    Ug = P.sb("Ug", [128, 6, 128], F32, nslots=6)
    Ugp = [P.sb("Ugp", [128, 6, 128], BF16, nslots=6) for _ in range(3)]
    gp = P.sb("gp", [128, 3, 6], BF16, nslots=3)
    gpf = P.sb("gpf", [128, 2, 6], F32, nslots=2)
    gr_ = P.sb("gr_", [128, 2, 6], F32, nslots=2)
    NONESb = P.sb("NONESb", [128, 128], BF16)
    P.memset('pool', NONESb[:], -1.0)
    MUb = P.sb("MUb", [128, 2, 128], BF16)
    P.cp('dve', MUb[:, 0, :], gc_[:, 0, 0, :])
    P.cp('dve', MUb[:, 1, :], gc_[:, 1, 0, :])
    Bdb = P.sb("Bdb", [128, 6, 128], BF16, nslots=6)
    T1 = P.sb("T1", [128, 6, 128], F32, nslots=2)
    D1 = P.sb("D1", [128, 6, 128], BF16, nslots=2)
    D2 = P.sb("D2", [128, 6, 128], BF16, nslots=2)
    D1s = P.sb("D1s", [128, 6, 128], BF16, nslots=2)
    D2s = P.sb("D2s", [128, 6, 128], BF16, nslots=2)
    sc = P.sb("scg", [128, 8, 6], F32, nslots=8)
    EGR = P.sb("EGR", [64, 6, 128], BF16, nslots=2)
    qdT = P.sb("qdT", [64, 6, 128], BF16)
    kbT = P.sb("kbT", [64, 6, 128], BF16, nslots=2)
    XX = [P.sb("XX", [128, 6, 128], BF16, nslots=2) for _ in range(2)]
    XT = [P.sb("XTg", [128, 6, 128], BF16, nslots=2) for _ in range(2)]
    TT = [P.sb("TTg", [128, 6, 128], BF16, nslots=2) for _ in range(2)]
    inT = P.sb("inT", [128, 6, 128], BF16, nslots=2)
    kbe = P.sb("kbe", [128, 6, 64], BF16)
    vb = P.sb("vb", [128, 6, 64], BF16)
    kdec = P.sb("kdec", [128, 6, 64], BF16)
    wTb = P.sb("wTb", [64, 6, 128], BF16, nslots=2)
    uu = P.sb("uu", [128, 6, 64], F32)
    vnew = P.sb("vnew", [128, 6, 64], BF16)
    of = [P.sb("of", [128, 384], F32) for _ in range(2)]
    osum = P.sb("osum", [128, 6, 64], F32)
    osq = P.sb("osq", [128, 6, 64], F32)
    ssum = P.sb("ssum", [128, 6], F32)
    zs = [P.sb("zs", [128, 384], BF16) for _ in range(2)]
    ogd = P.sb("ogd", [128, 384], BF16)
    ostg = [P.sb("ostg", [128, 3, 128], BF16) for _ in range(2)]
    PTo = P.ps("PTo", [128, 3, 128], BF16)
    pai = [0]

    class _V:
        def __init__(self, b):
            self.b = b

        def __getitem__(self, idx):
            if not isinstance(idx, tuple):
                idx = (idx,)
            if len(idx) == 1:
                return self.b[idx[0], 0:3, :]
            return self.b[idx]

    def pa():
        pai[0] += 1
        return _V(PA[pai[0] % 5])

    it = 0
    for dr in range(2 if 'noG2' not in c.stages else 0):
        MU, M0, M1, NSU, NSL = [gc_[:, dr, i, :] for i in range(5)]
        for (u0, s) in c.units:
            P.memset('pool', S[:], 0.0)
            P.memset('pool', Sb[:], 0.0)
            nck = s // 128
            order = range(nck) if dr == 0 else range(nck - 1, -1, -1)
            for ck in order:
                r0 = u0 + ck * 128
                q_, k_, km, vm = qT[it % 2], kT[it % 2], ktm[it % 2], vtm[it % 2]
                P.dma('sp', q_[:], QKT[0:384, r0:r0 + 128].rearrange("(h d) t -> d h t", d=64))
                P.dma('sp', k_[:], QKT[384:768, r0:r0 + 128].rearrange("(h d) t -> d h t", d=64))
                P.dma('sp', km[:].rearrange("p a b -> p (a b)"), KVTM[0][r0:r0 + 128, :])
                P.dma('sp', vm[:].rearrange("p a b -> p (a b)"), KVTM[1][r0:r0 + 128, :])
                beta = GB[:, r0 // 128, dr * 6:dr * 6 + 6]
                gg = GB[:, r0 // 128, 12 + dr * 6:12 + dr * 6 + 6]
                P.cp('dve', gp[:, 0, :], gg)
                P.cp('dve', gpf[:, 0, :], gp[:, 0, :])
                P.tt('dve', gr_[:, 0, :], gg, gpf[:, 0, :], ALU.subtract)
                P.cp('dve', gp[:, 1, :], gr_[:, 0, :])
                P.cp('dve', gpf[:, 1, :], gp[:, 1, :])
                P.tt('dve', gr_[:, 1, :], gr_[:, 0, :], gpf[:, 1, :], ALU.subtract)
                P.cp('dve', gp[:, 2, :], gr_[:, 1, :])
                for pc_ in range(3):
                    P.tt('pool', Ugp[pc_][:], B3(MUb[:, dr, :]), gp[:, pc_, :].unsqueeze(2).broadcast_to([128, 6, 128]), ALU.mult)
                P.tt('pool', Bdb[:], B3(ident[:]), beta.unsqueeze(2).broadcast_to([128, 6, 128]), ALU.mult)
                for gr in range(2):
                    dps = pa()
                    for hh in range(3):
                        h = gr * 3 + hh
                        for pc_ in range(3):
                            P.mm(dps[:, hh, :], ONESb[:], Ugp[pc_][:, h, :], start=(pc_ == 0), stop=False)
                        for pc_ in range(3):
                            P.mm(dps[:, hh, :], Ugp[pc_][:, h, :], NONESb[:], start=False, stop=(pc_ == 2))
                    hs = slice(gr * 3, gr * 3 + 3)
                    P.tt('dve', T1[:, hs, :], dps[:], B3(M0, 3), ALU.min)
                    P.act(D1[:, hs, :], T1[:, hs, :], AF.Exp)
                    P.tt('dve', T1[:, hs, :], dps[:], B3(M1, 3), ALU.max)
                    P.act(D2[:, hs, :], T1[:, hs, :], AF.Exp, scale=-1.0)
                    P.tt('pool', D1s[:, hs, :], D1[:, hs, :], B3(NSU, 3), ALU.mult)
                    P.tt('pool', D2s[:, hs, :], D2[:, hs, :], B3(NSL, 3), ALU.mult)
                gcp = pa()
                gflat = gcp[:].rearrange("p a b -> p (a b)")
                for pc_ in range(3):
                    P.mm(gflat[:, 0:6], MUb[:, dr, :], gp[:, pc_, :], start=(pc_ == 0), stop=(pc_ == 2))
                for pc_ in range(3):
                    P.mm(gflat[:, 8:14], ONESb[:], gp[:, pc_, :], start=(pc_ == 0), stop=(pc_ == 2))
                P.cp('act', sc[:, 0, :], gflat[:, 0:6])
                P.act(sc[:, 1, :], gflat[:, 0:6], AF.Exp)
                P.cp('act', sc[:, 2, :], gflat[:, 8:14])
                P.act(sc[:, 3, :], gflat[:, 8:14], AF.Exp)
                P.tt('pool', sc[:, 4, :], sc[:, 2, :], sc[:, 0, :], ALU.subtract)
                P.act(sc[:, 5, :], sc[:, 4, :], AF.Exp)
                P.tt('pool', sc[:, 6, :], beta, sc[:, 1, :], ALU.mult)
                for gr in range(2):
                    hs = slice(gr * 3, gr * 3 + 3)
                    gr2 = pa()
                    br2 = pa()
                    for hh in range(3):
                        h = gr * 3 + hh
                        for pc_ in range(3):
                            P.mm(gr2[0:64, hh, :], ONESb[:, 0:64], Ugp[pc_][:, h, :], start=(pc_ == 0), stop=(pc_ == 2))
                        P.mm(br2[0:64, hh, :], ONESb[:, 0:64], Bdb[:, h, :])
                    P.act(EGR[:, hs, :], gr2[0:64, 0:3, :], AF.Exp)
                    P.tt('dve', kbT[:, hs, :], k_[:, hs, :], br2[0:64, 0:3, :], ALU.mult)
                P.tt('pool', qdT[:], q_[:], EGR[:], ALU.mult)
                X, Xt, Tt = XX[0], XT[0], TT[0]
                for gr in range(2):
                    a1, a2, a3 = pa(), pa(), pa()
                    hs = slice(gr * 3, gr * 3 + 3)
                    for hh in range(3):
                        h = gr * 3 + hh
                        P.mm(a1[:, hh, :], k_[:, h, :], kbT[:, h, :])
                        P.mm(a2[:, hh, :], kbT[:, h, :], k_[:, h, :])
                        P.mm(a3[:, hh, :], k_[:, h, :], q_[:, h, :])
                    P.tt('dve', Xt[:, hs, :], a1[:], D1s[:, hs, :], ALU.mult)
                    P.tt('dve', X[:, hs, :], a2[:], D2s[:, hs, :], ALU.mult)
                    P.tt('dve', inT[:, hs, :], a3[:], D1[:, hs, :], ALU.mult)
                    P.tt('pool', Tt[:, hs, :], Xt[:, hs, :], B3(identb[:], 3), ALU.add)
                cur = 0
                for lvl in range(6):
                    X, Xt, Tt = XX[cur], XT[cur], TT[cur]
                    Xn, Xtn, Ttn = XX[1 - cur], XT[1 - cur], TT[1 - cur]
                    for gr in range(2):
                        hs = slice(gr * 3, gr * 3 + 3)
                        p1 = pa()
                        for hh in range(3):
                            h = gr * 3 + hh
                            P.mm(p1[:, hh, :], Xt[:, h, :], X[:, h, :])
                        P.cp('act', Xn[:, hs, :], p1[:])
                        if lvl < 5:
                            p2 = pa()
                            for hh in range(3):
                                h = gr * 3 + hh
                                P.mm(p2[:, hh, :], X[:, h, :], Xt[:, h, :])
                            P.cp('dve', Xtn[:, hs, :], p2[:])
                        p3 = pa()
                        for hh in range(3):
                            h = gr * 3 + hh
                            P.mm(p3[:, hh, :], Xn[:, h, :], Tt[:, h, :])
                        P.tt('dve', Ttn[:, hs, :], p3[:], Tt[:, hs, :], ALU.add)
                    cur = 1 - cur
                Tt = TT[cur]
                P.tt('pool', kbe[:], km[:], sc[:, 6, :].unsqueeze(2).broadcast_to([128, 6, 64]), ALU.mult)
                P.tt('pool', vb[:], vm[:], beta.unsqueeze(2).broadcast_to([128, 6, 64]), ALU.mult)
                P.tt('pool', kdec[:], km[:], sc[:, 5, :].unsqueeze(2).broadcast_to([128, 6, 64]), ALU.mult)
                pu = PB[0]
                for gr in range(2):
                    hs = slice(gr * 3, gr * 3 + 3)
                    pw = pa()
                    for hh in range(3):
                        h = gr * 3 + hh
                        P.mm(pw[0:64, hh, :], kbe[:, h, :], Tt[:, h, :])
                        P.mm(pu[:, h, :], Tt[:, h, :], vb[:, h, :])
                    P.cp('act', wTb[:, hs, :], pw[0:64, 0:3, :])
                P.cp('dve', uu[:], pu[:])
                pws = PB[1]
                for h in range(6):
                    P.mm(pws[:, h, :], wTb[:, h, :], Sb[:, h, :])
                P.tt('dve', vnew[:], uu[:], pws[:], ALU.subtract)
                po = PB[0]
                pds = PB[1]
                for h in range(6):
                    P.mm(po[:, h, :], qdT[:, h, :], Sb[:, h, :], start=True, stop=False)
                    P.mm(po[:, h, :], inT[:, h, :], vnew[:, h, :], start=False, stop=True)
                for h in range(6):
                    P.mm(pds[0:64, h, :], kdec[:, h, :], vnew[:, h, :])
                P.tt('pool', S[:], S[:], sc[0:64, 3, :].unsqueeze(2).broadcast_to([64, 6, 64]), ALU.mult)
                P.tt('dve', S[:], S[:], pds[0:64, :, :], ALU.add)
                P.cp('act', Sb[:], S[:])
                o_ = of[it % 2]
                if dr == 0:
                    P.cp('act', o_[:], po[:].rearrange("p a b -> p (a b)"))
                    P.dma('sp', OFD[r0:r0 + 128, :], o_[:])
                else:
                    P.dma('sp', o_[:], OFD[r0:r0 + 128, :])
                    z_ = zs[it % 2]
                    P.dma('sp', z_[:], TMV[r0:r0 + 128, 640:1024])
                    P.tt('dve', osum[:], po[:], o_[:].rearrange("p (a b) -> p a b", a=6), ALU.add)
                    P.tt('pool', osq[:], osum[:], osum[:], ALU.mult)
                    P.op('dve', lambda g: g.tensor_reduce(out=ssum[:], in_=osq[:], axis=mybir.AxisListType.X, op=ALU.add),
                         [osq[:]], [ssum[:]])
                    P.act(ssum[:], ssum[:], AF.Sqrt, bias=eps6[:], scale=1.0 / 64, extra=[eps6[:]])
                    P.op('dve', lambda g: g.reciprocal(out=ssum[:], in_=ssum[:]), [ssum[:]], [ssum[:]])
                    P.tt('dve', osum[:], osum[:], ssum[:].unsqueeze(2).broadcast_to([128, 6, 64]), ALU.mult)
                    P.tt('pool', osum[:], osum[:], nw[:].unsqueeze(1).broadcast_to([128, 6, 64]), ALU.mult)
                    P.tt('pool', ogd[:], osum[:].rearrange("p a b -> p (a b)"), z_[:], ALU.mult)
                    for cc in range(3):
                        P.tr(PTo[:, cc, :], ogd[:, cc * 128:(cc + 1) * 128], identb[:])
                    og = ostg[it % 2]
                    P.cp('act', og[:], PTo[:])
                    P.dma('sp', OBRT[384:768, r0:r0 + 128].rearrange("(cc p) t -> p cc t", p=128), og[:])
                it += 1
    P.pop()
    P.pop()


def layer_norm_tile(P, yt, outt, G, Bv, sc1, eps5, junk):
    P.op('act', lambda g: g.activation(junk[:], yt[:], AF.Copy, accum_out=sc1[:, 0:1]), [yt[:]], [junk[:], sc1[:, 0:1]])
    P.op('act', lambda g: g.activation(junk[:], yt[:], AF.Square, accum_out=sc1[:, 1:2]), [yt[:]], [junk[:], sc1[:, 1:2]])
    P.ts('dve', sc1[:, 2:3], sc1[:, 0:1], 1.0 / D, ALU.mult)
    P.tt('dve', sc1[:, 3:4], sc1[:, 2:3], sc1[:, 2:3], ALU.mult)
    P.stt('dve', sc1[:, 4:5], sc1[:, 1:2], 1.0 / D, sc1[:, 3:4], ALU.mult, ALU.subtract)
    P.act(sc1[:, 5:6], sc1[:, 4:5], AF.Sqrt, bias=eps5[:], extra=[eps5[:]])
    P.op('dve', lambda g: g.reciprocal(out=sc1[:, 6:7], in_=sc1[:, 5:6]), [sc1[:, 5:6]], [sc1[:, 6:7]])
    P.op('dve', lambda g: g.tensor_scalar(out=junk[:], in0=yt[:], scalar1=sc1[:, 2:3], scalar2=sc1[:, 6:7],
                                          op0=ALU.subtract, op1=ALU.mult), [yt[:], sc1[:]], [junk[:]])
    P.tt('pool', junk[:], junk[:], G[:], ALU.mult)
    P.tt('pool', outt[:], junk[:], Bv[:], ALU.add)


def stage_merge(P, c, l, xsrc, FMG, OBRT, X1, AFFS, wbr_in, wout_in, wr_in, ln_in, identb):
    P.push()
    WBR = P.sb("WBR", [128, 6, D], BF16, nslots=6)
    WO = P.sb("WO", [128, KC, D], BF16, nslots=KC)
    WR = P.sb("WR", [128, KC, 16], BF16)
    wst = [P.sb("wstm", [128, D], F32) for _ in range(2)]
    for kc in range(6):
        P.dma('sp', wst[kc % 2][:], wbr_in[l, kc * 128:(kc + 1) * 128, :])
        P.cp('dve', WBR[:, kc, :], wst[kc % 2][:])
    for kc in range(KC):
        P.dma('sp', wst[kc % 2][:], wout_in[l, kc * 128:(kc + 1) * 128, :])
        P.cp('dve', WO[:, kc, :], wst[kc % 2][:])
    wrs = P.sb("wrs", [128, KC, 16], F32)
    P.dma('sp', wrs[:], wr_in[l, :, :].rearrange("(kc p) e -> p kc e", p=128))
    P.cp('dve', WR[:], wrs[:])
    G = P.sb("lnG", [128, D], F32)
    Bv = P.sb("lnB", [128, D], F32)
    P.dma('sp', G[:], ln_in[l, 0, :, :])
    P.dma('sp', Bv[:], ln_in[l, 1, :, :])
    eps5 = P.sb("eps5", [128, 1], F32)
    P.memset('pool', eps5[:], 1e-5)
    ob = [P.sb("obm", [128, 6, 512], BF16) for _ in range(2)]
    gt = [P.sb("gtm", [128, 24, 512], BF16, nslots=3) for _ in range(2)]
    mg = P.sb("mgm", [128, KC, 512], BF16, nslots=KC)
    tq = [P.sb("tqm", [128, 512], F32) for _ in range(3)]
    xt = [P.sb("xtm", [128, D], F32) for _ in range(2)]
    yt = [P.sb("ytm", [128, D], F32) for _ in range(2)]
    x1 = [P.sb("x1m", [128, D], F32) for _ in range(2)]
    x1b = [P.sb("x1bm", [128, D], BF16) for _ in range(2)]
    x1T = P.sb("x1T", [128, KC, 128], BF16)
    junk = P.sb("junkm", [128, D], F32)
    sc1 = [P.sb("sc1m", [128, 8], F32) for _ in range(2)]
    ee = P.sb("eem", [128, 16], F32)
    PBr = [P.ps("PBr", [128, 512]) for _ in range(3)]
    PM = [P.ps("PMm", [128, 512]) for _ in range(2)]
    PTx = P.ps("PTx", [128, 4, 128], BF16)
    PL = P.ps("PLm", [128, 16])
    BRK = ((0, 1), (2,), (3, 4, 5))
    for bi in range(c.T // 512):
        r0 = bi * 512
        o_, g_ = ob[bi % 2], gt[bi % 2]
        P.dma('sp', o_[:], OBRT[:, r0:r0 + 512].rearrange("(cc p) t -> p cc t", p=128))
        for b in range(3):
            P.dma('sp', g_[:, b * 8:(b + 1) * 8, :], FMG[b * 1024:(b + 1) * 1024, r0:r0 + 512].rearrange("(cc p) t -> p cc t", p=128))
        for m in range(KC):
            for b in range(3):
                ks = BRK[b]
                for n, kc in enumerate(ks):
                    P.mm(PBr[b][:], WBR[:, kc, m * 128:(m + 1) * 128], o_[:, kc, :], start=(n == 0), stop=(n == len(ks) - 1))
                P.tt('dve', tq[b][:], PBr[b][:], g_[:, b * 8 + m, :], ALU.mult)
            P.tt('pool', tq[0][:], tq[0][:], tq[1][:], ALU.add)
            P.tt('pool', mg[:, m, :], tq[0][:], tq[2][:], ALU.add)
        for t in range(4):
            ti = bi * 4 + t
            x_, y_, o1, o1b, s1 = xt[ti % 2], yt[ti % 2], x1[ti % 2], x1b[ti % 2], sc1[ti % 2]
            P.dma('sp', x_[:], xsrc(r0 + t * 128))
            for hf in range(2):
                pm = PM[hf]
                for kc in range(KC):
                    P.mm(pm[:], mg[:, kc, t * 128:(t + 1) * 128], WO[:, kc, hf * 512:(hf + 1) * 512], start=(kc == 0), stop=(kc == KC - 1))
                P.stt('dve', y_[:, hf * 512:(hf + 1) * 512], x_[:, hf * 512:(hf + 1) * 512], c.alpha, pm[:], ALU.mult, ALU.add)
            if 'mA' in c.stages:
                continue
            layer_norm_tile(P, y_, o1, G, Bv, s1, eps5, junk)
            P.dma('sp', X1[r0 + t * 128:r0 + (t + 1) * 128, :], o1[:])
            if 'mB' in c.stages:
                continue
            P.cp('act', o1b[:], o1[:])
            for k2 in range(2):
                for k4 in range(4):
                    kc = k2 * 4 + k4
                    P.tr(PTx[:, k4, :], o1b[:, kc * 128:(kc + 1) * 128], identb[:])
                P.cp('dve', x1T[:, k2 * 4:(k2 + 1) * 4, :], PTx[:])
            if 'mC' in c.stages:
                continue
            for kc in range(KC):
                P.mm(PL[:], x1T[:, kc, :], WR[:, kc, :], start=(kc == 0), stop=(kc == KC - 1))
            if 'mD' in c.stages:
                continue
            P.act(ee[:], PL[:], AF.Exp)
            P.op('dve', lambda g, s1=s1: g.tensor_reduce(out=s1[:, 7:8], in_=ee[:], axis=mybir.AxisListType.X, op=ALU.add),
                 [ee[:]], [s1[:, 7:8]])
            P.op('dve', lambda g, s1=s1: g.reciprocal(out=s1[:, 7:8], in_=s1[:, 7:8]), [s1[:, 7:8]], [s1[:, 7:8]])
            P.ts('dve', AFFS[:, ti, :], ee[:], s1[:, 7:8], ALU.mult, extra=[s1[:]])
    P.pop()


def stage_moe(P, c, l, X1, AFFS, XE, YE, agin, agout, wup_in, wgate_in, wdown_in, ln_in, identb, dst):
    nc = P.nc
    E, DE = c.E, c.DE
    NPt = c.NP * c.SP // 128
    Tt = c.T // 128
    St = Tt - NPt
    SL = c.SL
    cap = (float(c.NP * c.SP), float(2 * c.SS // E))
    P.push()
    ONES = P.sb("ONESm", [128, 128], F32)
    P.memset('pool', ONES[:], 1.0)
    ONESb = P.sb("ONESmb", [128, 128], BF16)
    P.memset('pool', ONESb[:], 1.0)
    idx = P.sb("idx", [128, Tt, E], I32, nslots=Tt)
    mgt = P.sb("mgt", [128, Tt, E], F32, nslots=Tt)
    idx1 = [P.sb("idx1", [128, 1], I32) for _ in range(32)]
    bcreg = nc.gpsimd.to_reg(SL - 1)
    P.push()
    APA = P.sb("APA", [128, NCORES * NPt, E], F32)
    P.dma('sp', agin[:, :], AFFS[:, 0:NPt, :].rearrange("p a b -> p (a b)"))
    P.dma('pool', agout[:, :], agin[:, :], inc=1,
          fn=lambda g: g.collective_compute("AllGather", ALU.bypass, replica_groups=[list(range(NCORES))],
                                            ins=[agin.ap().opt()], outs=[agout.ap().opt()]))
    for r in range(NCORES):
        P.dma('sp', APA[:, r * NPt:(r + 1) * NPt, :].rearrange("p a b -> p (a b)"), agout[r * 128:(r + 1) * 128, :])
    nmax = max(NCORES * NPt, St)
    cmpb = P.sb("cmpb", [128, nmax, E], F32)
    lo = P.sb("lo", [128, 2, E], F32)
    hi = P.sb("hi", [128, 2, E], F32)
    mid = P.sb("mid", [128, 2, E], F32)
    cnt = P.sb("cnt", [128, E], F32)
    ge = P.sb("ge", [128, E], F32)
    dd = P.sb("dd", [128, E], F32)
    PC_ = P.ps("PCm", [128, E])
    P.memset('pool', lo[:], 0.0)
    P.memset('pool', hi[:], 1.0)
    for itn in range(30):
        P.tt('dve', mid[:], lo[:], hi[:], ALU.add)
        P.ts('dve', mid[:], mid[:], 0.5, ALU.mult)
        for g in range(2):
            A = APA[:, :, :] if g == 0 else AFFS[:, NPt:Tt, :]
            n = NCORES * NPt if g == 0 else St
            cb = cmpb[:, 0:n, :]
            P.tt('dve', cb, A, mid[:, g, :].unsqueeze(1).broadcast_to([128, n, E]), ALU.is_ge)
            P.op('dve', lambda gg, cb=cb: gg.tensor_reduce(out=cnt[:], in_=cb.rearrange("p n e -> p e n"),
                                                           axis=mybir.AxisListType.X, op=ALU.add), [cb], [cnt[:]])
            P.mm(PC_[:], ONES[:], cnt[:])
            P.ts('dve', ge[:], PC_[:], cap[g], ALU.is_ge)
            P.tt('dve', dd[:], mid[:, g, :], lo[:, g, :], ALU.subtract)
            P.tt('dve', dd[:], dd[:], ge[:], ALU.mult)
            P.tt('dve', lo[:, g, :], lo[:, g, :], dd[:], ALU.add)
            P.tt('dve', dd[:], hi[:, g, :], mid[:, g, :], ALU.subtract)
            P.tt('dve', dd[:], dd[:], ge[:], ALU.mult)
            P.tt('dve', hi[:, g, :], mid[:, g, :], dd[:], ALU.add)
    mk = P.sb("mk", [128, Tt, E], F32)
    mkb = P.sb("mkb", [128, Tt, E], BF16)
    P.tt('dve', mk[:, 0:NPt, :], AFFS[:, 0:NPt, :], lo[:, 0, :].unsqueeze(1).broadcast_to([128, NPt, E]), ALU.is_ge)
    P.tt('dve', mk[:, NPt:Tt, :], AFFS[:, NPt:Tt, :], lo[:, 1, :].unsqueeze(1).broadcast_to([128, St, E]), ALU.is_ge)
    P.tt('pool', mgt[:], mk[:], AFFS[:], ALU.mult)
    P.cp('pool', mkb[:], mk[:])
    SLT = P.sb("SLT", [128, 128], BF16)
    P.memset('pool', SLT[:], 1.0)
    P.op('pool', lambda g: g.affine_select(out=SLT[:], in_=SLT[:], pattern=[[1, 128]], compare_op=ALU.is_gt, fill=0.0,
                                           base=0, channel_multiplier=-1), [SLT[:]], [SLT[:]])
    pos = P.sb("pos", [128, Tt, E], F32, nslots=Tt)
    tot = P.sb("tot", [128, Tt, E], F32, nslots=Tt)
    base = P.sb("base", [128, Tt, E], F32, nslots=Tt)
    PP = [P.ps("PPm", [128, 512]) for _ in range(2)]
    flat = Tt * E
    mkf = mkb[:].rearrange("p a b -> p (a b)")
    for q in range(0, flat, 512):
        w = min(512, flat - q)
        P.mm(PP[0][:, 0:w], SLT[:], mkf[:, q:q + w])
        P.cp('act', pos[:].rearrange("p a b -> p (a b)")[:, q:q + w], PP[0][:, 0:w])
        P.mm(PP[1][:, 0:w], ONESb[:], mkf[:, q:q + w])
        P.cp('dve', tot[:].rearrange("p a b -> p (a b)")[:, q:q + w], PP[1][:, 0:w])
    P.memset('pool', base[:, 0, :], 0.0)
    for t in range(1, Tt):
        P.tt('pool', base[:, t, :], base[:, t - 1, :], tot[:, t - 1, :], ALU.add)
    P.tt('dve', pos[:], pos[:], base[:], ALU.add)
    P.ts('dve', pos[:], pos[:], -1.0e6, ALU.add)
    P.tt('dve', pos[:], pos[:], mk[:], ALU.mult)
    P.ts('dve', pos[:], pos[:], 1.0e6, ALU.add)
    P.cp('dve', idx[:], pos[:])
    P.pop()
    P.push()
    xf = [P.sb("xfd", [128, D], F32) for _ in range(2)]
    xb = [P.sb("xbd", [128, D], BF16) for _ in range(2)]
    for t in range(Tt):
        P.dma('sp', xf[t % 2][:], X1[t * 128:(t + 1) * 128, :])
        P.cp('dve' if t % 2 else 'act', xb[t % 2][:], xf[t % 2][:])
        for e in range(E):
            i1 = idx1[(t * E + e) % 32]
            P.cp('pool', i1[:], idx[:, t, e:e + 1])
            ia = i1[:, :]
            P.dma('pool', XE[e][:, :], xb[t % 2][:], extra_reads=[ia],
                  fn=lambda g, e=e, ia=ia, t=t: g.indirect_dma_start(
                      out=XE[e][:, :], out_offset=bass.IndirectOffsetOnAxis(ap=ia, axis=0),
                      in_=xb[t % 2][:], in_offset=None, bounds_check=bcreg, oob_is_err=False))
    P.pop()
    P.push()
    Wg = P.sb("Wg", [128, KC, DE], BF16, nslots=KC)
    Wu = P.sb("Wu", [128, KC, DE], BF16, nslots=KC)
    Wd = P.sb("Wd", [128, DE // 128, D], BF16, nslots=DE // 128)
    wsg = [P.sb("wsg", [128, DE], F32) for _ in range(2)]
    wsd = [P.sb("wsd", [128, D], F32) for _ in range(2)]
    xl = [P.sb("xle", [128, D], BF16) for _ in range(2)]
    xeT = P.sb("xeT", [128, KC, 512], BF16, nslots=4)
    hT = P.sb("hT", [128, DE // 128, 512], BF16, nslots=DE // 128)
    hs_ = [P.sb("hs_", [128, 512], F32) for _ in range(2)]
    ye = [P.sb("ye", [128, D], F32) for _ in range(2)]
    PTe = P.ps("PTe", [128, 4, 128], BF16)
    PG = [P.ps("PGe", [128, 512]) for _ in range(2)]
    PU = [P.ps("PUe", [128, 512]) for _ in range(2)]
    PY = [P.ps("PYe", [128, 512]) for _ in range(2)]
    k = 0
    for e in range(E):
        for kc in range(KC):
            P.dma('sp', wsg[k % 2][:], wgate_in[l, e, kc * 128:(kc + 1) * 128, :])
            P.cp('pool', Wg[:, kc, :], wsg[k % 2][:])
            k += 1
            P.dma('sp', wsg[k % 2][:], wup_in[l, e, kc * 128:(kc + 1) * 128, :])
            P.cp('pool', Wu[:, kc, :], wsg[k % 2][:])
            k += 1
        for fc in range(DE // 128):
            P.dma('sp', wsd[fc % 2][:], wdown_in[l, e, fc * 128:(fc + 1) * 128, :])
            P.cp('pool', Wd[:, fc, :], wsd[fc % 2][:])
        for sg in range(0, SL, 512):
            nst = min(4, (SL - sg) // 128)
            w = nst * 128
            for st in range(nst):
                x_ = xl[st % 2]
                P.dma('sp', x_[:], XE[e][sg + st * 128:sg + (st + 1) * 128, :])
                for k2 in range(2):
                    for k4 in range(4):
                        P.tr(PTe[:, k4, :], x_[:, (k2 * 4 + k4) * 128:(k2 * 4 + k4 + 1) * 128], identb[:])
                    P.cp('dve' if k2 else 'act', xeT[:, k2 * 4:(k2 + 1) * 4, st * 128:(st + 1) * 128], PTe[:])
            for fc in range(DE // 128):
                pg, pu = PG[fc % 2], PU[fc % 2]
                for kc in range(KC):
                    P.mm(pg[:, 0:w], Wg[:, kc, fc * 128:(fc + 1) * 128], xeT[:, kc, 0:w], start=(kc == 0), stop=(kc == KC - 1))
                for kc in range(KC):
                    P.mm(pu[:, 0:w], Wu[:, kc, fc * 128:(fc + 1) * 128], xeT[:, kc, 0:w], start=(kc == 0), stop=(kc == KC - 1))
                P.act(hs_[fc % 2][:, 0:w], pg[:, 0:w], AF.Silu)
                P.tt('dve', hT[:, fc, 0:w], hs_[fc % 2][:, 0:w], pu[:, 0:w], ALU.mult)
            for st in range(nst):
                y_ = ye[st % 2]
                for hf in range(2):
                    py = PY[hf]
                    for fc in range(DE // 128):
                        P.mm(py[:], hT[:, fc, st * 128:(st + 1) * 128], Wd[:, fc, hf * 512:(hf + 1) * 512],
                             start=(fc == 0), stop=(fc == DE // 128 - 1))
                    P.cp('act' if hf else 'dve', y_[:, hf * 512:(hf + 1) * 512], py[:])
                P.dma('sp', YE[e][sg + st * 128:sg + (st + 1) * 128, :], y_[:])
    P.pop()
    P.push()
    G = P.sb("ln2G", [128, D], F32)
    Bv = P.sb("ln2B", [128, D], F32)
    P.dma('sp', G[:], ln_in[l, 2, :, :])
    P.dma('sp', Bv[:], ln_in[l, 3, :, :])
    eps5 = P.sb("eps5b", [128, 1], F32)
    P.memset('pool', eps5[:], 1e-5)
    bufs = P.sb("bufs", [128, E, D], F32, nslots=E)
    P.memset('pool', bufs[:], 0.0)
    xf = [P.sb("xfc", [128, D], F32) for _ in range(2)]
    acc = [P.sb("accc", [128, D], F32) for _ in range(2)]
    o2 = [P.sb("o2c", [128, D], F32) for _ in range(2)]
    junk = P.sb("junkc", [128, D], F32)
    sc1 = [P.sb("sc1c", [128, 8], F32) for _ in range(2)]
    for t in range(Tt):
        x_, a_ = xf[t % 2], acc[t % 2]
        P.dma('sp', x_[:], X1[t * 128:(t + 1) * 128, :])
        for e in range(E):
            i1 = idx1[(t * E + e) % 32]
            P.cp('pool', i1[:], idx[:, t, e:e + 1])
            ia = i1[:, :]
            P.dma('pool', bufs[:, e, :], YE[e][:, :], extra_reads=[ia],
                  fn=lambda g, e=e, ia=ia: g.indirect_dma_start(
                      out=bufs[:, e, :], out_offset=None, in_=YE[e][:, :],
                      in_offset=bass.IndirectOffsetOnAxis(ap=ia, axis=0), bounds_check=bcreg, oob_is_err=False))
        P.ts('dve', a_[:], x_[:], c.alpha, ALU.mult)
        for e in range(E):
            P.stt('dve', a_[:], bufs[:, e, :], mgt[:, t, e:e + 1], a_[:], ALU.mult, ALU.add, extra=[mgt[:, t, :]])
        layer_norm_tile(P, a_, o2[t % 2], G, Bv, sc1[t % 2], eps5, junk)
        P.dma('sp', dst(t * 128), o2[t % 2][:])
    P.pop()
    P.pop()


def _rope_tab(SS):
    half = 32
    inv = (10000.0 ** (-np.arange(half, dtype=np.float32) / half)).astype(np.float32)
    pos = np.arange(SS, dtype=np.float32)
    ang = pos[None, :] * inv[:, None]
    cos = np.cos(ang).astype(np.float32)
    sin = np.sin(ang).astype(np.float32)
    tab = np.zeros((2, 128, SS), np.float32)
    for d in range(128):
        tab[0, d] = cos[d % 32]
        tab[1, d] = sin[d % 32]
    return tab


def _band_tab():
    k = np.arange(128)[:, None]
    q = np.arange(128)[None, :]
    return np.stack([(np.abs(128 * dl + k - q) <= 64).astype(np.float32) for dl in (-1, 0, 1)])


def _rot_mat():
    R = np.zeros((128, 128), np.float32)
    for m in range(128):
        if m % 64 < 32:
            R[m + 32, m] = -1.0
        else:
            R[m - 32, m] = 1.0
    return R


def host_shared(cfg, inputs):
    f32 = lambda a: np.ascontiguousarray(a, dtype=np.float32)
    L = cfg.L
    keys = na_keys(cfg)
    nab, nam = na_host_tables(f32(inputs['na_rpb']), keys)
    convw = f32(f32(inputs['conv_w']).reshape(L, 5, 9, 128).transpose(0, 3, 2, 1))
    small = f32(np.broadcast_to(np.concatenate([f32(inputs['a_log']).reshape(L, 12), f32(inputs['dt_bias']).reshape(L, 12)],
                                               1)[:, None, :], (L, 128, 24)))
    normw = f32(np.broadcast_to(f32(inputs['gdn_norm_w'])[:, None, :], (L, 128, 64)))
    wbr = f32(np.concatenate([inputs['w_br_na'], inputs['w_br_dil'], inputs['w_br_gdn']], axis=1))
    ln = f32(np.broadcast_to(np.stack([inputs['ln1_g'], inputs['ln1_b'], inputs['ln2_g'], inputs['ln2_b']], 1)[:, :, None, :],
                             (L, 4, 128, D)))
    return {
        "xs": f32(inputs['x_sample']).reshape(cfg.SS, D), "win": f32(inputs['w_in']), "rope": _rope_tab(cfg.SS),
        "rot": _rot_mat(), "ident": np.eye(128, dtype=np.float32), "nab": nab, "nam": nam, "band": _band_tab(),
        "gconst": gdn_host_consts(), "convw": convw, "small": small, "normw": normw, "wbr": wbr,
        "wout": f32(inputs['w_out']), "wr": f32(inputs['w_router']), "ln": ln,
        "wup": f32(inputs['w_up']), "wgate": f32(inputs['w_gate']), "wdown": f32(inputs['w_down']),
    }


def kernel(**inputs):
    cfg = Cfg(SP=4096, NP=2, SS=16384, E=16, DE=2048, L=2, stages='all', debug=False)
    xp = np.ascontiguousarray(inputs['x_prompt'], dtype=np.float32)
    nc, P = build(cfg)
    shared = host_shared(cfg, inputs)
    in_maps = []
    for c in range(NCORES):
        m = dict(shared)
        m["xp"] = xp[c * cfg.NP:(c + 1) * cfg.NP].reshape(cfg.NP * cfg.SP, D)
        in_maps.append(m)
    res = run_bass_kernel_spmd(nc, in_maps, core_ids=list(range(NCORES)))
    y_prompt = np.concatenate([np.asarray(r["yp"]).reshape(cfg.NP, cfg.SP, D) for r in res.results], axis=0)
    y_sample = np.asarray(res.results[0]["ys"]).reshape(1, cfg.SS, D)
    return (y_prompt.astype(np.float32), y_sample.astype(np.float32))
```
